# Optimizing a Trainium2 kernel written in Bass

```python
import jax, jax.numpy as jnp
from jax import lax
import numpy as np

D_MODEL = 1024
BATCH = 8
SEQ = 4096
DEPTH = 1
DEC_BATCH = 32
DEC_SEQ = 32
PAST_LEN = 2048

CHUNK = 64
N_PAST_CHUNKS = 8
N_BAND = N_PAST_CHUNKS + 1
D_CONV = D_MODEL // 2
D_ATT = D_MODEL - D_CONV
HEAD_DIM = 64
N_HEADS = D_ATT // HEAD_DIM
CONV_W = 3
REL_CLIP = 128
N_REL = 2 * REL_CLIP + 1
D_PLE = 256
EPS = 1e-6
SPLITS = (D_CONV, D_CONV, D_CONV, D_CONV, D_ATT, D_ATT, D_ATT, D_ATT)
D_IN = sum(SPLITS)

kernel_name = "hybrid_conv_chunkattn_streaming_step"


def rmsnorm(x, g):
    xf = x.astype(jnp.float32)
    r = lax.rsqrt(jnp.mean(xf * xf, axis=-1, keepdims=True) + EPS)
    return (xf * r * g.astype(jnp.float32)).astype(x.dtype)


def project_in(xn, w_in):
    z = xn @ w_in
    offsets = [int(o) for o in np.cumsum(SPLITS)[:-1]]
    return jnp.split(z, offsets, axis=-1)


def short_conv(u_ext, w, length):
    out = w[0] * u_ext[:, 0:length]
    for t in range(1, CONV_W):
        out = out + w[t] * u_ext[:, t:t + length]
    return out


def band_attention(q, k, v, rel, valid, rel_bias):
    bias = rel_bias[:, jnp.clip(rel, -REL_CLIP, REL_CLIP) + REL_CLIP]
    s = jnp.einsum('...qhd,...khd->...hqk', q, k).astype(jnp.float32) * (HEAD_DIM ** -0.5)
    s = s + bias.astype(jnp.float32)
    if valid is not None:
        s = jnp.where(valid, s, -1e30)
    pr = jax.nn.softmax(s, axis=-1).astype(v.dtype)
    return jnp.einsum('...hqk,...khd->...qhd', pr, v)


def merge_out(yc, ya, norm_conv, norm_att, w_out):
    y = jnp.concatenate([rmsnorm(yc, norm_conv), rmsnorm(ya, norm_att)], axis=-1)
    return y @ w_out


def per_layer_embed(x, p, ple_norm, w_ple_gate, w_ple_proj):
    gate = jax.nn.sigmoid(rmsnorm(x, ple_norm) @ w_ple_gate)
    return x + gate * (p @ w_ple_proj)


def prompt_layer(x, p, norm_in, w_in, conv_w, rel_bias, norm_conv, norm_att, w_out,
                 ple_norm, w_ple_gate, w_ple_proj):
    B, S, _ = x.shape
    xn = rmsnorm(x, norm_in)
    h, bg, cg, zc, q, k, v, za = project_in(xn, w_in)
    u = cg * h
    u_ext = jnp.pad(u, ((0, 0), (CONV_W - 1, 0), (0, 0)))
    yc = bg * short_conv(u_ext, conv_w, S) * jax.nn.silu(zc)
    conv_state = u_ext[:, S:]
    nc = S // CHUNK
    qc = q.reshape(B, nc, CHUNK, N_HEADS, HEAD_DIM)
    pad = ((0, 0), (N_PAST_CHUNKS, 0), (0, 0), (0, 0), (0, 0))
    kp = jnp.pad(k.reshape(B, nc, CHUNK, N_HEADS, HEAD_DIM), pad)
    vp = jnp.pad(v.reshape(B, nc, CHUNK, N_HEADS, HEAD_DIM), pad)
    k_band = jnp.concatenate([kp[:, o:o + nc] for o in range(N_BAND)], axis=2)
    v_band = jnp.concatenate([vp[:, o:o + nc] for o in range(N_BAND)], axis=2)
    chunk_id = jnp.arange(nc)[:, None] - N_PAST_CHUNKS + jnp.arange(N_BAND)[None, :]
    valid = jnp.repeat(chunk_id >= 0, CHUNK, axis=1)[None, :, None, None, :]
    slot = jnp.arange(N_BAND * CHUNK)
    rel = (N_BAND - 1) * CHUNK + jnp.arange(CHUNK)[:, None] - slot[None, :]
    ya = band_attention(qc, k_band, v_band, rel, valid, rel_bias).reshape(B, S, D_ATT)
    ya = ya * jax.nn.silu(za)
    win = min(N_PAST_CHUNKS * CHUNK, S)
    k_state = k.reshape(B, S, N_HEADS, HEAD_DIM)[:, S - win:]
    v_state = v.reshape(B, S, N_HEADS, HEAD_DIM)[:, S - win:]
    x = x + merge_out(yc, ya, norm_conv, norm_att, w_out)
    x = per_layer_embed(x, p, ple_norm, w_ple_gate, w_ple_proj)
    return x, k_state, v_state, conv_state


def sample_layer(x, p, cache_k, cache_v, conv_buf, norm_in, w_in, conv_w, rel_bias, norm_conv,
                 norm_att, w_out, ple_norm, w_ple_gate, w_ple_proj):
    B, S, _ = x.shape
    xn = rmsnorm(x, norm_in)
    h, bg, cg, zc, q, k, v, za = project_in(xn, w_in)
    u = cg * h
    u_ext = jnp.concatenate([conv_buf.astype(u.dtype), u], axis=1)
    yc = bg * short_conv(u_ext, conv_w, S) * jax.nn.silu(zc)
    conv_state = u_ext[:, S:]
    qh = q.reshape(B, S, N_HEADS, HEAD_DIM)
    kh = k.reshape(B, S, N_HEADS, HEAD_DIM)
    vh = v.reshape(B, S, N_HEADS, HEAD_DIM)
    kv_win = cache_k.shape[1]
    k_all = jnp.concatenate([cache_k.astype(kh.dtype), kh], axis=1)
    v_all = jnp.concatenate([cache_v.astype(vh.dtype), vh], axis=1)
    kpos = jnp.concatenate([jnp.arange(kv_win) - kv_win, jnp.arange(S)])
    rel = jnp.arange(S)[:, None] - kpos[None, :]
    ya = band_attention(qh, k_all, v_all, rel, None, rel_bias).reshape(B, S, D_ATT)
    ya = ya * jax.nn.silu(za)
    x = x + merge_out(yc, ya, norm_conv, norm_att, w_out)
    x = per_layer_embed(x, p, ple_norm, w_ple_gate, w_ple_proj)
    return x, kh, vh, conv_state


def setup_inputs(seed: int = 0) -> dict:
    key = jax.random.key(seed)
    ks = jax.random.split(key, 20)
    kv_win = min(N_PAST_CHUNKS * CHUNK, PAST_LEN)
    f32 = jnp.float32
    nrm = lambda k, shape, s: jax.random.normal(k, shape, f32) * s
    return {
        "x_prompt": nrm(ks[0], (BATCH, SEQ, D_MODEL), 1.0),
        "x_sample": nrm(ks[1], (DEC_BATCH, DEC_SEQ, D_MODEL), 1.0),
        "cache_k": nrm(ks[2], (DEPTH, DEC_BATCH, kv_win, N_HEADS, HEAD_DIM), 1.0),
        "cache_v": nrm(ks[3], (DEPTH, DEC_BATCH, kv_win, N_HEADS, HEAD_DIM), 1.0),
        "state_conv": nrm(ks[4], (DEPTH, DEC_BATCH, CONV_W - 1, D_CONV), 1.0),
        "p_prompt": nrm(ks[5], (DEPTH, BATCH, SEQ, D_PLE), 1.0),
        "p_sample": nrm(ks[6], (DEPTH, DEC_BATCH, DEC_SEQ, D_PLE), 1.0),
        "norm_in": 1.0 + nrm(ks[7], (DEPTH, D_MODEL), 0.02),
        "w_in": nrm(ks[8], (DEPTH, D_MODEL, D_IN), D_MODEL ** -0.5),
        "conv_w": nrm(ks[9], (DEPTH, CONV_W, D_CONV), CONV_W ** -0.5),
        "rel_bias": nrm(ks[10], (DEPTH, N_HEADS, N_REL), 0.5),
        "norm_conv": 1.0 + nrm(ks[11], (DEPTH, D_CONV), 0.02),
        "norm_att": 1.0 + nrm(ks[12], (DEPTH, D_ATT), 0.02),
        "w_out": nrm(ks[13], (DEPTH, D_CONV + D_ATT, D_MODEL), (D_CONV + D_ATT) ** -0.5),
        "ple_norm": 1.0 + nrm(ks[14], (DEPTH, D_MODEL), 0.02),
        "w_ple_gate": nrm(ks[15], (DEPTH, D_MODEL, D_MODEL), D_MODEL ** -0.5),
        "w_ple_proj": nrm(ks[16], (DEPTH, D_PLE, D_MODEL), D_PLE ** -0.5),
        "final_norm": 1.0 + nrm(ks[17], (D_MODEL,), 0.02),
    }


def reference(x_prompt, x_sample, cache_k, cache_v, state_conv, p_prompt, p_sample, norm_in, w_in,
              conv_w, rel_bias, norm_conv, norm_att, w_out, ple_norm, w_ple_gate, w_ple_proj,
              final_norm):
    xp, xs = x_prompt, x_sample
    kp_l, vp_l, cp_l, ks_l, vs_l, cs_l = [], [], [], [], [], []
    for i in range(DEPTH):
        xp, kpi, vpi, cpi = prompt_layer(xp, p_prompt[i], norm_in[i], w_in[i], conv_w[i], rel_bias[i],
                                         norm_conv[i], norm_att[i], w_out[i], ple_norm[i],
                                         w_ple_gate[i], w_ple_proj[i])
        xs, ksi, vsi, csi = sample_layer(xs, p_sample[i], cache_k[i], cache_v[i], state_conv[i],
                                         norm_in[i], w_in[i], conv_w[i], rel_bias[i], norm_conv[i],
                                         norm_att[i], w_out[i], ple_norm[i], w_ple_gate[i],
                                         w_ple_proj[i])
        kp_l.append(kpi); vp_l.append(vpi); cp_l.append(cpi)
        ks_l.append(ksi); vs_l.append(vsi); cs_l.append(csi)
    y_prompt = rmsnorm(xp, final_norm)
    y_sample = rmsnorm(xs, final_norm)
    k_prompt = jnp.stack(kp_l)
    v_prompt = jnp.stack(vp_l)
    conv_prompt = jnp.stack(cp_l)
    k_sample = jnp.stack(ks_l)
    v_sample = jnp.stack(vs_l)
    conv_sample = jnp.stack(cs_l)
    return (y_prompt, y_sample, k_prompt, v_prompt, conv_prompt, k_sample, v_sample, conv_sample)
```

```python
import os
import numpy as np
from contextlib import ExitStack
import concourse.bass as bass
import concourse.mybir as mybir
from concourse.bass_utils import run_bass_kernel_spmd

F32 = mybir.dt.float32
BF16 = mybir.dt.bfloat16
AF = mybir.ActivationFunctionType
ALU = mybir.AluOpType

D = 1024
SEQ = 4096
NG = 8
EPS = 1e-6
N_CORES = 8


class _Stop(Exception):
    pass


_KSTOP = float(os.environ.get("KSTOP", "0"))


def _chk(level):
    if _KSTOP and _KSTOP <= level:
        raise _Stop()


class Prog:
    def __init__(self, nc, es):
        self.nc, self.es = nc, es
        self.E = dict(pe=nc.tensor, act=nc.scalar, dve=nc.vector, pool=nc.gpsimd, sp=nc.sync)
        self.esem, self.ecnt = {}, {}
        for k in ("pe", "act", "dve", "pool"):
            self.esem[k] = es.enter_context(nc.semaphore("es_" + k))
            self.ecnt[k] = 0
        self.dsem, self.dcnt = {}, {}
        self.waited = {}
        self.res = {}
        self.out_tokens = []

    def _wait(self, eng, tok):
        sem, name, val, src = tok
        if src == "pe" and eng == "pe":
            return
        key = (eng, name)
        if self.waited.get(key, 0) >= val:
            return
        self.E[eng].wait_ge(sem, val)
        self.waited[key] = val

    def op(self, eng, fn, reads=(), writes=(), dma=None, is_out=False):
        deps = []
        for r in reads:
            e = self.res.get(r)
            if e and e[0]:
                deps.append(e[0])
        for w in writes:
            e = self.res.get(w)
            if e:
                if e[0]:
                    deps.append(e[0])
                deps.extend(e[1].values())
        for t in deps:
            self._wait(eng, t)
        ins = fn()
        if dma is not None:
            if dma not in self.dsem:
                self.dsem[dma] = self.es.enter_context(self.nc.semaphore("ds_" + dma))
                self.dcnt[dma] = 0
            self.dcnt[dma] += 16
            ins.then_inc(self.dsem[dma], 16)
            tok = (self.dsem[dma], "d_" + dma, self.dcnt[dma], "dma")
        else:
            self.ecnt[eng] += 1
            ins.then_inc(self.esem[eng], 1)
            tok = (self.esem[eng], eng, self.ecnt[eng], eng)
        for r in reads:
            self.res.setdefault(r, [None, {}])[1][tok[1]] = tok
        for w in writes:
            self.res[w] = [tok, {}]
        if is_out:
            self.out_tokens.append(tok)
        return tok

    def finish(self, eng="sp"):
        best = {}
        for t in self.out_tokens:
            if t[1] not in best or best[t[1]][2] < t[2]:
                best[t[1]] = t
        for t in best.values():
            self.waited.pop((eng, t[1]), None)
            self.E[eng].wait_ge(t[0], t[2])


def build_program():
    nc = bass.Bass("TRN2", target_bir_lowering=False)
    di = lambda n, s: nc.dram_tensor(n, s, F32, kind="ExternalInput").ap()
    do = lambda n, s: nc.dram_tensor(n, s, F32, kind="ExternalOutput").ap()
    xp = di("xp", [4096, 1024]); pp = di("pp", [4096, 256])
    xsm = di("xsm", [128, 1024]); psm = di("psm", [128, 256])
    ck = di("ck", [4, 512, 512]); cv = di("cv", [4, 512, 512]); sc = di("sc", [4, 2, 512])
    norm_in = di("norm_in", [1024]); w_in = di("w_in", [1024, 4096]); conv_w = di("conv_w", [3, 512])
    rel_bias = di("rel_bias", [8, 257]); norm_conv = di("norm_conv", [512]); norm_att = di("norm_att", [512])
    w_out = di("w_out", [1024, 1024]); ple_norm = di("ple_norm", [1024]); w_g = di("w_g", [1024, 1024])
    w_p = di("w_p", [256, 1024]); final_norm = di("final_norm", [1024]); ident = di("ident", [128, 128])
    yp = do("yp", [4096, 1024]); ysm = do("ysm", [128, 1024]); kp = do("kp", [512, 512]); vp = do("vp", [512, 512])
    cp = do("cp", [2, 512]); ksn = do("ksn", [128, 512]); vsn = do("vsn", [128, 512]); csn = do("csn", [4, 2, 512])

    with ExitStack() as es:
        P = Prog(nc, es)
        sbt = lambda n, s, d: es.enter_context(nc.sbuf_tensor(n, s, d))
        Win = sbt("Win", [128, 8, 4096], BF16)
        Wout = sbt("Wout", [128, 8, 1024], BF16)
        Wg = sbt("Wg", [128, 8, 1024], BF16)
        Wp = sbt("Wp", [128, 2, 1024], BF16)
        xin = sbt("xin", [128, 2, 1024], F32)
        x1 = sbt("x1", [128, 2, 1024], F32)
        gate = sbt("gate", [128, 512], F32)
        xsb2 = sbt("xsb", [128, 2, 1024], BF16)
        xsT = sbt("xsT", [128, 8, 512], BF16)
        utmp = sbt("utmp", [128, 1, 514], F32)
        hist = sbt("hist", [128, 4, 2], F32)
        htmp = sbt("htmp", [128, 512], F32)
        ctmp = sbt("ctmp", [128, 512], F32)
        ttmp = sbt("ttmp", [128, 512], F32)
        ycT = sbt("ycT", [128, 2, 4, 512], BF16)
        sq = sbt("sq", [128, 1, 512], BF16)
        QT = sbt("QT", [128, 4, 512], BF16)
        KT = sbt("KT", [128, 4, 1024], BF16)
        Vp = sbt("Vp", [128, 8, 4, 192], BF16)
        zas = sbt("zas", [128, 4, 512], BF16)
        PT = sbt("PT", [128, 4, 512], BF16)
        Sb = sbt("Sb", [128, 2, 256], F32)
        yaT = sbt("yaT", [128, 4, 512], BF16)
        Thi = sbt("Thi", [128, 8, 256], BF16)
        Tlo = sbt("Tlo", [128, 8, 256], BF16)
        xs1T2 = sbt("xs1T", [128, 2, 8, 128], BF16)
        pin2 = sbt("pin", [128, 2, 256], F32)
        pbf2 = sbt("pbf", [128, 2, 256], BF16)
        pT2 = sbt("pT", [128, 2, 2, 128], BF16)
        identb = sbt("identb", [128, 128], BF16)
        ones = sbt("ones", [128, 8], BF16)
        gf = sbt("gf", [128, 1024], F32)
        gin = sbt("gin", [128, 8], F32)
        gy = sbt("gy", [128, 8], F32)
        gple = sbt("gple", [128, 8], F32)
        cw = sbt("cw", [128, 4, 3], F32)
        cb = sbt("cb", [128, 8], F32)
        cbm = sbt("cbm", [128, 8], F32)
        scu = sbt("scu", [128, 4, 4, 2], F32)
        cso = sbt("cso", [128, 4, 4, 2], F32)
        sm = sbt("sm", [128, 64], F32)
        cst = sbt("cst", [128, 32], F32)
        banks = [es.enter_context(nc.psum_tensor(f"bk{i}", [128, 512], F32)) for i in range(8)]
        trb = banks[7][:].bitcast(BF16)
        yan = xs1T2[:, 0, :, :].rearrange("p c t -> p (c t)").bitcast(F32)
        Rr = xs1T2[:, 1, :, :].rearrange("p c t -> p (c t)").bitcast(F32)
        kst, vst = yan, Rr

        V_, S_, T_, G_ = nc.vector, nc.scalar, nc.tensor, nc.gpsimd
        Ttab = Wg[:, 0:4, :].rearrange("p a b -> p (a b)").bitcast(F32).rearrange("p (h x) -> p h x", x=256)
        TKEYS = [("Wg", dc) for dc in range(4)]
        gctr = [0]

        def gb():
            b = gctr[0] % 7
            gctr[0] += 1
            return b

        def dma(q, name, out, in_, reads=(), writes=(), slow=False, is_out=False):
            if slow:
                fn = lambda: P.E[q].dma_start(out=out, in_=in_, allow_slow_non_contiguous=True)
            else:
                fn = lambda: P.E[q].dma_start(out=out, in_=in_)
            return P.op(q, fn, reads=reads, writes=writes, dma=name, is_out=is_out)

        dma("sp", "c_id", x1[:, 0, 0:128], ident, writes=[("x1", 0)])
        P.op("dve", lambda: V_.tensor_copy(out=identb[:], in_=x1[:, 0, 0:128]), reads=[("x1", 0)], writes=[("ident",)])
        P.op("dve", lambda: V_.memset(ones[:], 1.0), writes=[("ones",)])
        P.op("dve", lambda: V_.memset(cst[:, 0:16], float(D * EPS)), writes=[("cst", 0)])
        P.op("dve", lambda: V_.memset(cst[:, 16:32], -0.5), writes=[("cst", 1)])
        P.op("dve", lambda: V_.memset(hist[:], 0.0), writes=[("hist",)])
        P.op("dve", lambda: V_.memset(Vp[:, :, :, 64:128], 1.0), writes=[("Vones",)])
        dma("pool", "c_small", gin[:], norm_in.rearrange("(c p) -> p c", p=128), writes=[("gin",)], slow=True)
        dma("pool", "c_small2", gy[:, 0:4], norm_conv.rearrange("(c p) -> p c", p=128), writes=[("gy0",)], slow=True)
        dma("pool", "c_small3", gy[:, 4:8], norm_att.rearrange("(c p) -> p c", p=128), writes=[("gy1",)], slow=True)
        dma("pool", "c_small4", gple[:], ple_norm.rearrange("(c p) -> p c", p=128), writes=[("gple",)], slow=True)
        for tt_ in range(3):
            dma("pool", f"c_small5{tt_}", cw[:, :, tt_], conv_w[tt_].rearrange("(j p) -> p j", p=128), writes=[("cw", tt_)], slow=True)
        dma("pool", "c_small6", cb[:], bass.AP(rel_bias.tensor, 256, [[0, 128], [257, 8]]), writes=[("cb",)], slow=True)
        for j_ in range(4):
            for t2 in range(2):
                dma("pool", f"c_sc{j_}{t2}", scu[:, j_, :, t2], sc[:, t2, j_ * 128:(j_ + 1) * 128].rearrange("s p -> p s"), writes=[("scu", j_, t2)], slow=True)
        dma("pool", "c_small7", gf[:], bass.AP(final_norm.tensor, 0, [[0, 128], [1, 1024]]), writes=[("gf",)])
        def rsqrt_eps(dst, src, rkey, wkey):
            P.op("pool", lambda: G_.tensor_tensor(out=dst, in0=src, in1=cst[:, 0:1], op=ALU.add), reads=[rkey, ("cst", 0)], writes=[wkey])
            P.op("pool", lambda: G_.tensor_tensor(out=dst, in0=dst, in1=cst[:, 16:17], op=ALU.pow), reads=[wkey, ("cst", 1)], writes=[wkey])

        def run_pipelined(gens, skew, max_active):
            pending, active, since = list(gens), [], skew
            while pending or active:
                if pending and len(active) < max_active and (since >= skew or not active):
                    lk = pending[0][1]
                    if all(lk.isdisjoint(a_[1]) for a_ in active):
                        active.append(pending.pop(0)); since = 0
                for a_ in list(active):
                    try:
                        r_ = next(a_[0])
                        if isinstance(r_, tuple) and r_[0] == "rel":
                            a_[1].discard(r_[1])
                    except StopIteration:
                        active.remove(a_)
                since += 1

        def stage_a(src_ap, slot, col0, ssq_c, tbank=None, use_pt=False, half7=False):
            if use_pt:
                xsb = PT[:, 0:2, :].rearrange("p a b -> p (a b)")
                XKS = [("PT", 0), ("PT", 1)]
            else:
                xsb = xsb2[:, slot, :]
                XKS = [("xsb", slot)]
            XK = XKS[0]
            dma("sp", f"xin{slot}", xin[:, slot, :], src_ap, writes=[("xin", slot)])
            yield
            P.op("act", lambda: S_.activation(out=xsb, in_=xin[:, slot, :], func=AF.Square, accum_out=sm[:, ssq_c:ssq_c + 1]),
                 reads=[("xin", slot)], writes=XKS + [("sm", ssq_c)])
            yield
            rsqrt_eps(sm[:, ssq_c + 1:ssq_c + 2], sm[:, ssq_c:ssq_c + 1], ("sm", ssq_c), ("sm", ssq_c + 1))
            yield
            if half7:
                P.op("dve", lambda: V_.tensor_scalar(out=xsb, in0=xin[:, slot, :], scalar1=sm[:, ssq_c + 1:ssq_c + 2], scalar2=None, op0=ALU.mult),
                     reads=[("xin", slot), ("sm", ssq_c + 1)], writes=XKS)
            else:
                P.op("act", lambda: S_.activation(out=xsb, in_=xin[:, slot, :], func=AF.Copy, scale=sm[:, ssq_c + 1:ssq_c + 2]),
                     reads=[("xin", slot), ("sm", ssq_c + 1)], writes=XKS)
            yield

            if half7:
                for hf in range(2):
                    def trh(hf=hf):
                        last = None
                        for dc in range(4):
                            c = hf * 4 + dc
                            last = T_.transpose(trb[:, dc * 128:(dc + 1) * 128], xsb[:, c * 128:(c + 1) * 128], identb[:])
                        return last
                    P.op("pe", trh, reads=XKS + [("ident",)], writes=[("bank", 7)])
                    P.op("dve", lambda: V_.tensor_scalar(out=xsT[:, hf * 4:hf * 4 + 4, col0:col0 + 128], in0=trb[:, 0:512].rearrange("p (c t) -> p c t", c=4),
                                                         scalar1=32.0, scalar2=None, op0=ALU.mult), reads=[("bank", 7)], writes=[("xsT", col0 // 128)])
                    yield
                return
            tb = gb() if tbank is None else tbank
            tv = banks[tb][:].bitcast(BF16)

            def tr():
                last = None
                for dc in range(8):
                    last = T_.transpose(tv[:, dc * 128:(dc + 1) * 128], xsb[:, dc * 128:(dc + 1) * 128], identb[:])
                return last
            P.op("pe", tr, reads=XKS + [("ident",)], writes=[("bank", tb)])
            P.op("act", lambda: S_.activation(out=xsT[:, :, col0:col0 + 128], in_=tv.rearrange("p (c t) -> p c t", c=8), func=AF.Copy, scale=32.0),
                 reads=[("bank", tb)], writes=[("xsT", col0 // 128)])
            yield

        def proj_F(fidx, N, ntile, b=None):
            if b is None:
                b = gb()
            fq = fidx // 8
            f0 = fidx * 128

            def mm():
                last = None
                for dc in range(8):
                    last = T_.matmul(banks[b][:, 0:N], lhsT=Win[:, dc, f0:f0 + 128], rhs=xsT[:, dc, 0:N], start=(dc == 0), stop=(dc == 7))
                return last
            P.op("pe", mm, reads=WIN_ALL(fq) + [("xsT", t) for t in range(ntile)], writes=[("bank", b)])
            return b

        def proj_T(col0, M, fbase, b=None):
            if b is None:
                b = gb()
            fq = fbase // 1024

            def mm():
                last = None
                for dc in range(8):
                    last = T_.matmul(banks[b][0:M, :], lhsT=xsT[:, dc, col0:col0 + M], rhs=Win[:, dc, fbase:fbase + 512], start=(dc == 0), stop=(dc == 7))
                return last
            P.op("pe", mm, reads=WIN_ALL(fq) + [("xsT", col0 // 128)], writes=[("bank", b)])
            return b

        def conv_branch(N, ntile, uview, hview, prompt_hist, g, cbanks=None, ys=0):
            cctr = [0]

            def nb():
                if cbanks is None:
                    return gb()
                cctr[0] += 1
                return cbanks[cctr[0] % len(cbanks)]
            for j in range(4):
                us = 0
                U = utmp[:, us, :]
                bh = proj_F(j, N, ntile, nb())
                P.op("act", lambda bh=bh: S_.copy(out=htmp[:, 0:N], in_=banks[bh][:, 0:N]), reads=[("bank", bh)], writes=[("htmp",)])
                yield
                bc = proj_F(8 + j, N, ntile, nb())
                if prompt_hist:
                    P.op("dve", lambda U=U, j=j: V_.tensor_copy(out=U[:, 0:2], in_=hist[:, j, :]), reads=[("hist", j), ("hist",)], writes=[("utmp", us)])
                    yield
                    P.op("dve", lambda U=U, bc=bc: V_.tensor_tensor(out=U[:, 2:2 + N], in0=banks[bc][:, 0:N], in1=htmp[:, 0:N], op=ALU.mult),
                         reads=[("bank", bc), ("htmp",)], writes=[("utmp", us)])
                    yield
                    P.op("dve", lambda U=U, j=j: V_.tensor_copy(out=hist[:, j, :], in_=U[:, N:N + 2]), reads=[("utmp", us)], writes=[("hist", j)])
                    yield
                    u3 = lambda a, b_: U[:, a:a + N]
                    c3 = ctmp[:, 0:N]
                else:
                    Uv = uview(U)
                    P.op("dve", lambda Uv=Uv, j=j: V_.tensor_copy(out=Uv[:, :, 0:2], in_=scu[:, j, :, :]), reads=[("scu", j, 0), ("scu", j, 1)], writes=[("utmp", us, "h", 0), ("utmp", us, "h", 1)])
                    yield
                    P.op("dve", lambda Uv=Uv, bc=bc: V_.tensor_tensor(out=Uv[:, :, 2:34], in0=hview(banks[bc][:, 0:N]), in1=hview(htmp[:, 0:N]), op=ALU.mult),
                         reads=[("bank", bc), ("htmp",), ("utmp", us, "h", 0), ("utmp", us, "h", 1)], writes=[("utmp", us)])
                    yield
                    P.op("dve", lambda Uv=Uv, j=j: V_.tensor_copy(out=cso[:, j, :, :], in_=Uv[:, :, 32:34]), reads=[("utmp", us)], writes=[("cso", j)])
                    yield
                    u3 = lambda a, b_, Uv=Uv: Uv[:, :, a:a + 32]
                    c3 = hview(ctmp[:, 0:N])
                P.op("dve", lambda u3=u3, c3=c3, j=j: V_.tensor_scalar(out=c3, in0=u3(0, 0), scalar1=cw[:, j, 0:1], scalar2=None, op0=ALU.mult),
                     reads=[("utmp", us), ("cw", 0), ("cw", 1), ("cw", 2)], writes=[("ctmp",)])
                yield
                P.op("dve", lambda u3=u3, c3=c3, j=j: V_.scalar_tensor_tensor(out=c3, in0=u3(1, 0), scalar=cw[:, j, 1:2], in1=c3, op0=ALU.mult, op1=ALU.add),
                     reads=[("utmp", us), ("ctmp",)], writes=[("ctmp",)])
                yield
                P.op("dve", lambda u3=u3, c3=c3, j=j: V_.scalar_tensor_tensor(out=c3, in0=u3(2, 0), scalar=cw[:, j, 2:3], in1=c3, op0=ALU.mult, op1=ALU.add),
                     reads=[("utmp", us), ("ctmp",)], writes=[("ctmp",)])
                yield
                bb = proj_F(4 + j, N, ntile, nb())
                P.op("dve", lambda bb=bb: V_.tensor_tensor(out=ctmp[:, 0:N], in0=banks[bb][:, 0:N], in1=ctmp[:, 0:N], op=ALU.mult),
                     reads=[("bank", bb), ("ctmp",)], writes=[("ctmp",)])
                yield
                bz = proj_F(12 + j, N, ntile, nb())
                P.op("act", lambda bz=bz: S_.activation(out=ttmp[:, 0:N], in_=banks[bz][:, 0:N], func=AF.Tanh, scale=0.5), reads=[("bank", bz)], writes=[("ttmp",)])
                yield
                P.op("dve", lambda bz=bz: V_.scalar_tensor_tensor(out=ttmp[:, 0:N], in0=ttmp[:, 0:N], scalar=1.0, in1=banks[bz][:, 0:N], op0=ALU.add, op1=ALU.mult),
                     reads=[("bank", bz), ("ttmp",)], writes=[("ttmp",)])
                yield
                P.op("dve", lambda j=j: V_.tensor_tensor(out=ycT[:, ys, j, 0:N], in0=ttmp[:, 0:N], in1=ctmp[:, 0:N], op=ALU.mult),
                     reads=[("ttmp",), ("ctmp",)], writes=[("ycT", ys, j)])
                yield
                P.op("act", lambda j=j: S_.activation(out=sq[:, 0, 0:N], in_=ycT[:, ys, j, 0:N], func=AF.Square), reads=[("ycT", ys, j)], writes=[("sq", 0)])
                yield

                def ssq(j=j):
                    last = None
                    for t in range(ntile):
                        last = T_.matmul(banks[7][:, 256 + t:257 + t], lhsT=sq[:, 0, t * 128:(t + 1) * 128], rhs=ones[:, 0:1],
                                         start=(j == 0 and t == 0), stop=(j == 3), skip_group_check=True)
                    return last
                P.op("pe", ssq, reads=[("sq", 0), ("ones",)], writes=[("bank", 7)])
                yield

        def qkz_branch(N, ntile, kcol0, cbanks=None, kdst=None):
            cctr = [0]

            def nb():
                if cbanks is None:
                    return gb()
                cctr[0] += 1
                return cbanks[cctr[0] % len(cbanks)]
            for j in range(4):
                bq = proj_F(16 + j, N, ntile, nb())
                P.op("act", lambda: S_.activation(out=QT[:, j, 0:N], in_=banks[bq][:, 0:N], func=AF.Copy, scale=0.125),
                     reads=[("bank", bq)], writes=[("QT", j)])
                yield
                bk = proj_F(20 + j, N, ntile, nb())
                if kdst is None:
                    P.op("dve", lambda: V_.tensor_copy(out=KT[:, j, kcol0:kcol0 + N], in_=banks[bk][:, 0:N]),
                         reads=[("bank", bk)], writes=[("KT", kcol0 // 512, j)])
                else:
                    P.op("dve", lambda: V_.tensor_copy(out=kdst(j), in_=banks[bk][:, 0:N]), reads=[("bank", bk)], writes=[("Knew", j)])
                yield
                bz = proj_F(28 + j, N, ntile, nb())
                P.op("act", lambda: S_.activation(out=ttmp[:, 0:N], in_=banks[bz][:, 0:N], func=AF.Tanh, scale=0.5), reads=[("bank", bz)], writes=[("ttmp",)])
                yield
                P.op("dve", lambda: V_.scalar_tensor_tensor(out=zas[:, j, 0:N], in0=ttmp[:, 0:N], scalar=1.0, in1=banks[bz][:, 0:N], op0=ALU.add, op1=ALU.mult),
                     reads=[("bank", bz), ("ttmp",)], writes=[("zas", j)])
                yield

        def v_evac(b, M, vslot, wkey, dst=None):
            src = banks[b][0:M, :].rearrange("p (j e d) -> p j e d", e=2, d=64)
            if dst is not None:
                P.op("pool", lambda: G_.memset(dst[:, :, 64:128], 1.0), reads=[], writes=[wkey + (2,)])
                P.op("act", lambda: S_.copy(out=dst[:, :, 0:64], in_=src[:, :, 0, :]), reads=[("bank", b)], writes=[wkey + (0,), ("brd", b)])
                P.op("dve", lambda: V_.tensor_copy(out=dst[:, :, 128:192], in_=src[:, :, 1, :]), reads=[("bank", b), ("brd", b)], writes=[wkey + (1,), ("brd", b)])
                return
            P.op("act", lambda: S_.copy(out=Vp[0:M, vslot, :, 0:64], in_=src[:, :, 0, :]), reads=[("bank", b), ("Vones",)], writes=[wkey + (0,), ("brd", b)])
            P.op("dve", lambda: V_.tensor_copy(out=Vp[0:M, vslot, :, 128:192], in_=src[:, :, 1, :]), reads=[("bank", b), ("Vones",), ("brd", b)], writes=[wkey + (1,), ("brd", b)])

        def normalize_pair(j, bA, bB, ncols, col0):
            A, B = banks[bA], banks[bB]
            P.op("dve", lambda: V_.reciprocal(out=Rr[0:64, 0:ncols], in_=A[64:128, 0:ncols]), reads=[("bank", bA)], writes=[("Rr", 0), ("xs1T", 1)])
            yield
            P.op("dve", lambda: V_.reciprocal(out=Rr[64:128, 0:ncols], in_=B[0:64, 0:ncols]), reads=[("bank", bB)], writes=[("Rr", 1), ("xs1T", 1)])
            yield
            P.op("dve", lambda: V_.tensor_tensor(out=yan[0:64, 0:ncols], in0=A[0:64, 0:ncols], in1=Rr[0:64, 0:ncols], op=ALU.mult),
                 reads=[("bank", bA), ("Rr", 0), ("xs1T", 1)], writes=[("yan", 0), ("xs1T", 0)])
            yield
            P.op("dve", lambda: V_.tensor_tensor(out=yan[64:128, 0:ncols], in0=B[64:128, 0:ncols], in1=Rr[64:128, 0:ncols], op=ALU.mult),
                 reads=[("bank", bB), ("Rr", 1), ("xs1T", 1)], writes=[("yan", 1), ("xs1T", 0)])
            yield
            P.op("dve", lambda: V_.tensor_tensor(out=yaT[:, j, col0:col0 + ncols], in0=yan[:, 0:ncols], in1=zas[:, j, col0:col0 + ncols], op=ALU.mult),
                 reads=[("yan", 0), ("yan", 1), ("xs1T", 0), ("zas", j)], writes=[("yaT", j)])
            yield

        def att_ssq(N, ntile):
            for j in range(4):
                P.op("act", lambda j=j: S_.activation(out=sq[:, 0, 0:N], in_=yaT[:, j, 0:N], func=AF.Square), reads=[("yaT", j)], writes=[("sq", 0)])

                def ssq(j=j):
                    last = None
                    for t in range(ntile):
                        last = T_.matmul(banks[7][:, 260 + t:261 + t], lhsT=sq[:, 0, t * 128:(t + 1) * 128], rhs=ones[:, 0:1],
                                         start=False, stop=(j == 3), skip_group_check=True)
                    return last
                P.op("pe", ssq, reads=[("sq", 0), ("ones",)], writes=[("bank", 7)])
            P.op("dve", lambda: V_.tensor_scalar(out=sm[:, 16:24], in0=banks[7][:, 256:264], scalar1=float(2048 * EPS), scalar2=float(1.0 / 512.0), op0=ALU.add, op1=ALU.mult),
                 reads=[("bank", 7)], writes=[("sm_r",)])
            P.op("pool", lambda: G_.tensor_tensor(out=sm[:, 16:24], in0=sm[:, 16:24], in1=cst[:, 16:24], op=ALU.pow), reads=[("sm_r",), ("cst", 1)], writes=[("sm_r",)])

        def merge_ple(t, xsrc, psrc, ydst, xs_slot, ys=0, mb=None):
            c0 = t * 128
            sl = xs_slot
            X = x1[:, sl, :]
            xsb = xsb2[:, sl, :]
            XK = ("xsb", sl)
            xs1T = xs1T2[:, sl, :, :]
            pin, pbf, pT = pin2[:, sl, :], pbf2[:, sl, :], pT2[:, sl, :, :]
            m0 = 40 + 8 * sl
            dma("sp", f"xr{sl}", X, xsrc, writes=[("x1", sl)])
            dma("sp", f"pin{sl}", pin, psrc, writes=[("pin", sl)])
            yield
            for hh in range(2):
                d0 = hh * 512
                bc, ba = mb if mb else (gb(), gb())

                def mmc(bc=bc, d0=d0):
                    last = None
                    for j in range(4):
                        last = T_.matmul(banks[bc][:], lhsT=ycT[:, ys, j, c0:c0 + 128], rhs=Wout[:, j, d0:d0 + 512], start=(j == 0), stop=(j == 3))
                    return last

                def mma(ba=ba, d0=d0):
                    last = None
                    for j in range(4):
                        last = T_.matmul(banks[ba][:], lhsT=yaT[:, j, c0:c0 + 128], rhs=Wout[:, 4 + j, d0:d0 + 512], start=(j == 0), stop=(j == 3))
                    return last
                P.op("pe", mmc, reads=[("ycT", ys, j) for j in range(4)] + [("Wout", j) for j in range(4)], writes=[("bank", bc)])
                P.op("pe", mma, reads=[("yaT", j) for j in range(4)] + [("Wout", 4 + j) for j in range(4)], writes=[("bank", ba)])
                yield
                P.op("dve", lambda: V_.scalar_tensor_tensor(out=X[:, d0:d0 + 512], in0=banks[bc][:], scalar=sm[:, 16 + t:17 + t], in1=X[:, d0:d0 + 512],
                                                            op0=ALU.mult, op1=ALU.add), reads=[("bank", bc), ("sm_r",), ("x1", sl)], writes=[("x1", sl)])
                yield
                P.op("dve", lambda: V_.scalar_tensor_tensor(out=X[:, d0:d0 + 512], in0=banks[ba][:], scalar=sm[:, 20 + t:21 + t], in1=X[:, d0:d0 + 512],
                                                            op0=ALU.mult, op1=ALU.add), reads=[("bank", ba), ("sm_r",), ("x1", sl)], writes=[("x1", sl)])
                yield
            P.op("act", lambda: S_.activation(out=xsb, in_=X, func=AF.Square, accum_out=sm[:, m0:m0 + 1]), reads=[("x1", sl)], writes=[XK, ("sm", m0)])
            yield
            rsqrt_eps(sm[:, m0 + 1:m0 + 2], sm[:, m0:m0 + 1], ("sm", m0), ("sm", m0 + 1))
            yield
            P.op("act", lambda: S_.activation(out=xsb, in_=X, func=AF.Copy, scale=sm[:, m0 + 1:m0 + 2]), reads=[("x1", sl), ("sm", m0 + 1)], writes=[XK])
            yield

            tb = mb[1] if mb else gb()
            tv = banks[tb][:].bitcast(BF16)

            def tr():
                last = None
                for dc in range(8):
                    last = T_.transpose(tv[:, dc * 128:(dc + 1) * 128], xsb[:, dc * 128:(dc + 1) * 128], identb[:])
                return last
            P.op("pe", tr, reads=[XK, ("ident",)], writes=[("bank", tb)])
            P.op("act", lambda: S_.activation(out=xs1T, in_=tv.rearrange("p (c t) -> p c t", c=8), func=AF.Copy, scale=32.0), reads=[("bank", tb)], writes=[("xs1T", sl)])
            yield
            P.op("dve", lambda: V_.tensor_copy(out=pbf, in_=pin), reads=[("pin", sl)], writes=[("pbf", sl)])
            yield
            bp = mb[0] if mb else gb()
            pview = banks[bp][:].bitcast(BF16)

            def trp():
                last = None
                for c in range(2):
                    last = T_.transpose(pview[:, c * 128:(c + 1) * 128], pbf[:, c * 128:(c + 1) * 128], identb[:])
                return last
            P.op("pe", trp, reads=[("pbf", sl), ("ident",)], writes=[("bank", bp)])
            yield
            P.op("dve", lambda: V_.tensor_copy(out=pT, in_=pview[:, 0:256].rearrange("p (c t) -> p c t", c=2)), reads=[("bank", bp)], writes=[("pT", sl)])
            yield
            for hh in range(2):
                d0 = hh * 512
                bg, bq = mb if mb else (gb(), gb())

                def mmg(bg=bg, d0=d0):
                    last = None
                    for dc in range(8):
                        last = T_.matmul(banks[bg][:], lhsT=xs1T[:, dc, :], rhs=Wg[:, dc, d0:d0 + 512], start=(dc == 0), stop=(dc == 7))
                    return last

                def mmp(bq=bq, d0=d0):
                    last = None
                    for c in range(2):
                        last = T_.matmul(banks[bq][:], lhsT=pT[:, c, :], rhs=Wp[:, c, d0:d0 + 512], start=(c == 0), stop=(c == 1))
                    return last
                P.op("pe", mmg, reads=[("xs1T", sl)] + [("Wg", dc) for dc in range(8)], writes=[("bank", bg)])
                P.op("pe", mmp, reads=[("pT", sl), ("Wp", 0), ("Wp", 1)], writes=[("bank", bq)])
                yield
                P.op("act", lambda: S_.activation(out=gate[:], in_=banks[bg][:], func=AF.Tanh, scale=0.5), reads=[("bank", bg)], writes=[("gate",)])
                P.op("dve", lambda: V_.scalar_tensor_tensor(out=gate[:], in0=gate[:], scalar=1.0, in1=banks[bq][:], op0=ALU.add, op1=ALU.mult),
                     reads=[("bank", bq), ("gate",)], writes=[("gate",)])
                P.op("dve", lambda: V_.scalar_tensor_tensor(out=X[:, d0:d0 + 512], in0=gate[:], scalar=0.5, in1=X[:, d0:d0 + 512], op0=ALU.mult, op1=ALU.add),
                     reads=[("gate",), ("x1", sl)], writes=[("x1", sl)])
                yield
            P.op("act", lambda: S_.activation(out=xsb, in_=X, func=AF.Square, accum_out=sm[:, m0 + 2:m0 + 3]), reads=[("x1", sl)], writes=[XK, ("sm", m0 + 2)])
            yield
            rsqrt_eps(sm[:, m0 + 3:m0 + 4], sm[:, m0 + 2:m0 + 3], ("sm", m0 + 2), ("sm", m0 + 3))
            yield
            P.op("dve", lambda: V_.tensor_scalar(out=sm[:, m0 + 3:m0 + 4], in0=sm[:, m0 + 3:m0 + 4], scalar1=32.0, scalar2=None, op0=ALU.mult), reads=[("sm", m0 + 3)], writes=[("sm", m0 + 3)])
            P.op("dve", lambda: V_.scalar_tensor_tensor(out=X, in0=X, scalar=sm[:, m0 + 3:m0 + 4], in1=gf[:], op0=ALU.mult, op1=ALU.mult),
                 reads=[("x1", sl), ("sm", m0 + 3), ("gf",)], writes=[("x1", sl)])
            yield
            dma("pool", f"yst{sl}", ydst, X, reads=[("x1", sl)], is_out=True)
            yield

        Sbf = Sb[:, :, :].rearrange("p a b -> p (a b)")

        def proj_chain(g1, cbanks, parts=("c", "q", "v")):
            if "c" in parts:
                yield from conv_branch(512, 4, None, None, True, g1, cbanks, g1 % 2)
            if "q" in parts:
                yield from qkz_branch(512, 4, (g1 % 2) * 512, cbanks)
            for t in (range(4) if "v" in parts else ()):
                gt = 4 * g1 + t
                bv = proj_T(t * 128, 128, 3072, None if cbanks is None else cbanks[t % 2])
                v_evac(bv, 128, gt % 8, ("Vp", gt % 8))
                yield
                if g1 == NG - 1:
                    P.op("act", lambda: S_.copy(out=Sbf, in_=banks[bv][:]), reads=[("bank", bv), ("brd", bv)], writes=[("Sb", 0), ("Sb", 1)])
                    dma("pool", "vst", vp[t * 128:(t + 1) * 128, :], Sbf, reads=[("Sb", 0), ("Sb", 1)], is_out=True)
                    yield
                    bk2 = proj_T(t * 128, 128, 2560, None if cbanks is None else cbanks[(t + 1) % 2])
                    P.op("act", lambda: S_.copy(out=Sbf, in_=banks[bk2][:]), reads=[("bank", bk2)], writes=[("Sb", 0), ("Sb", 1)])
                    dma("pool", "kst", kp[t * 128:(t + 1) * 128, :], Sbf, reads=[("Sb", 0), ("Sb", 1)], is_out=True)
                    yield
            if g1 == NG - 1 and "v" in parts:
                for t2 in range(2):
                    dma("pool", f"cpst{t2}", cp[t2].rearrange("(j p) -> p j", p=128), hist[:, :, t2], reads=[("hist", j) for j in range(4)], slow=True, is_out=True)
                yield

        uview = lambda U: U[:, 0:136].rearrange("p (s c) -> p s c", c=34)
        hview = lambda a: a.rearrange("p (s c) -> p s c", c=32)

        def newV(s_):
            if s_ < 2:
                return ycT[0:32, 0, :, 128 + 192 * s_:128 + 192 * (s_ + 1)]
            return QT[0:32, :, 128 + 192 * (s_ - 2):128 + 192 * (s_ - 1)]

        def sample_chain(cbanks):
            yield from conv_branch(128, 1, uview, hview, False, 0, cbanks, 0)
            for j_ in range(4):
                for t2 in range(2):
                    dma("pool", f"cs_st{j_}{t2}", csn[:, t2, j_ * 128:(j_ + 1) * 128].rearrange("s p -> p s"), cso[:, j_, :, t2],
                        reads=[("cso", j_)], slow=True, is_out=True)
            yield
            yield from qkz_branch(128, 1, 512, cbanks, lambda j: zas[:, j, 128:256])
            bk2 = proj_T(0, 128, 2560, cbanks[0])
            P.op("act", lambda: S_.copy(out=Sbf, in_=banks[bk2][:]), reads=[("bank", bk2)], writes=[("Sb", 0), ("Sb", 1)])
            dma("pool", "kst", ksn, Sbf, reads=[("Sb", 0), ("Sb", 1)], is_out=True)
            yield
            for s_ in range(4):
                bv = proj_T(32 * s_, 32, 3072, cbanks[1 + s_ % 2])
                v_evac(bv, 32, None, ("Vnew", s_), newV(s_))
                P.op("act", lambda: S_.copy(out=Sbf[0:32, :], in_=banks[bv][0:32, :]), reads=[("bank", bv), ("brd", bv)], writes=[("Sb", 0), ("Sb", 1)])
                dma("pool", "vst", vsn[32 * s_:32 * s_ + 32, :], Sbf[0:32, :], reads=[("Sb", 0), ("Sb", 1)], is_out=True)
                yield

        run_pipelined([(stage_a(xp[t * 128:(t + 1) * 128, :], t % 2, t * 128, 2 * (t % 2)), {("s", t % 2)}) for t in range(4)], 2, 2)
        P.op("dve", lambda: V_.memset(Ttab, 0.0), writes=[("Ttab",)] + TKEYS)
        for h in range(8):
            P.op("dve", lambda h=h: V_.tensor_scalar(out=Ttab[:, h, :], in0=Ttab[:, h, :], scalar1=cb[:, h:h + 1],
                                                     scalar2=None, op0=ALU.add), reads=[("cb",)], writes=[("Ttab",)] + TKEYS)
        tt = None
        e = P.res.get(("Ttab",))
        P._wait("pool", e[0])
        if "toe" not in P.dsem:
            P.dsem["toe"] = es.enter_context(nc.semaphore("ds_toe"))
            P.dcnt["toe"] = 0
        for p in range(128):
            L = min(256, 129 + p)
            G_.dma_start(out=Ttab[p:p + 1, :, 0:L],
                         in_=bass.AP(rel_bias.tensor, 128 - p, [[0, 1], [257, 8], [1, L]])).then_inc(P.dsem["toe"], 16)
            P.dcnt["toe"] += 16
        P.res[("Ttab",)] = [(P.dsem["toe"], "d_toe", P.dcnt["toe"], "dma"), {}]

        stage_slots = [(x1, 0, "x1"), (x1, 1, "x1"), (xin, 0, "xin"), (xin, 1, "xin")]
        wctr = [0]
        cast_engs = ["act", "dve"]

        def load_w(src_ap, dst_ap, gain_ap, gkey, dkey):
            i = wctr[0]; wctr[0] += 1
            t, s, tk = stage_slots[i % 4]
            dma("sp", f"wst{i % 4}", t[:, s, :], src_ap, writes=[(tk, s)])
            eng = cast_engs[i % 2]
            rd = [(tk, s)] + ([gkey] if gkey else [])
            if gain_ap is None:
                if eng == "dve":
                    fn = lambda: V_.tensor_copy(out=dst_ap, in_=t[:, s, :])
                elif eng == "act":
                    fn = lambda: S_.copy(out=dst_ap, in_=t[:, s, :])
                else:
                    fn = lambda: G_.tensor_copy(out=dst_ap, in_=t[:, s, :])
            else:
                if eng == "dve":
                    fn = lambda: V_.tensor_scalar(out=dst_ap, in0=t[:, s, :], scalar1=gain_ap, scalar2=None, op0=ALU.mult)
                elif eng == "act":
                    fn = lambda: S_.activation(out=dst_ap, in_=t[:, s, :], func=AF.Copy, scale=gain_ap)
                else:
                    fn = lambda: G_.tensor_scalar(out=dst_ap, in0=t[:, s, :], scalar1=gain_ap, scalar2=None, op0=ALU.mult)
            P.op(eng, fn, reads=rd, writes=[dkey])

        WIN_ALL = lambda fq: [("Win", fq, dc) for dc in range(8)]
        for fq in range(4):
            for dc in range(8):
                load_w(w_in[dc * 128:(dc + 1) * 128, fq * 1024:(fq + 1) * 1024], Win[:, dc, fq * 1024:(fq + 1) * 1024],
                       gin[:, dc:dc + 1], ("gin",), ("Win", fq, dc))
            if fq == 1:
                for _ in proj_chain(0, None, ("c",)):
                    pass
            if fq == 3:
                for _ in proj_chain(0, None, ("q", "v")):
                    pass
        for fc in range(8):
            load_w(w_out[fc * 128:(fc + 1) * 128, :], Wout[:, fc, :], gy[:, fc:fc + 1], ("gy0",) if fc < 4 else ("gy1",), ("Wout", fc))
        for dc in range(4, 8):
            load_w(w_g[dc * 128:(dc + 1) * 128, :], Wg[:, dc, :], gple[:, dc:dc + 1], ("gple",), ("Wg", dc))
        for c in range(2):
            load_w(w_p[c * 128:(c + 1) * 128, :], Wp[:, c, :], None, None, ("Wp", c))

        WIN_ALL = lambda fq: [("Win", fq, dc) for dc in range(8)]

        try:
            _chk(1)
            for g in range(NG):
                tok0 = g * 512
                kslot = g % 2
                _chk(4)
                if g == 0:
                    P.op("dve", lambda: V_.memset(Ttab[64:128, :, 0:64], -30000.0), reads=[], writes=[("Ttab",)] + TKEYS)
                    P.op("dve", lambda: V_.tensor_copy(out=cbm[:], in_=cb[:]), reads=[("cb",)], writes=[("cbm",)])
                    P.op("dve", lambda: V_.memset(cbm[0:64, :], -30000.0), reads=[], writes=[("cbm",)])
                    P.op("act", lambda: S_.copy(out=Thi[:], in_=Ttab), reads=[("Ttab",)] + TKEYS, writes=[("Thi",)])
                    P.op("dve", lambda: V_.tensor_tensor(out=Tlo[:], in0=Ttab, in1=Thi[:], op=ALU.subtract), reads=[("Ttab",), ("Thi",)] + TKEYS, writes=[("Tlo",)])
                    for dc in range(4):
                        load_w(w_g[dc * 128:(dc + 1) * 128, :], Wg[:, dc, :], gple[:, dc:dc + 1], ("gple",), ("Wg", dc))
                border = [3, 4, 0, 1, 2, 5, 6, 7] if g > 0 else [4, 5, 6, 7]
                pend_norm = [None]
                a_pending, a_active = [], []
                if g + 1 < NG:
                    for t in range(4):
                        n0 = tok0 + 512 + t * 128
                        a_pending.append(stage_a(xp[n0:n0 + 128, :], t % 2, t * 128, 2 * (t % 2), None, False, True))
                else:
                    a_pending.append(stage_a(xsm, 0, 0, 0, None, False, True))

                def adv_a():
                    while a_pending and len(a_active) < 2:
                        a_active.append(a_pending.pop(0))
                    if a_active:
                        g_ = a_active.pop(0)
                        try:
                            next(g_)
                            a_active.append(g_)
                        except StopIteration:
                            pass
                for j in range(4):
                    ybank = [3 + 2 * (j % 2), 4 + 2 * (j % 2)]
                    steps = [(b, e) for b in border for e in (0, 1)]
                    pend = []

                    def step_geo(si):
                        b, e = steps[si]
                        cq_lo = max(0, 2 * b - 8); cq_hi = min(7, 2 * b + 1)
                        Nb = 64 * (cq_hi - cq_lo + 1)
                        ksl = ((g - 1) % 2) if b < 4 else (g % 2)
                        kc0 = ksl * 512 + (b % 4) * 128
                        ntoe = 64 * max(0, min(cq_hi, 2 * b - 5) - cq_lo + 1)
                        x0 = 64 * (8 + cq_lo - 2 * b)
                        return b, e, Nb, ksl, kc0, 64 * cq_lo, ntoe, x0

                    def do_qk2(s0, j=j):
                        for si in (s0, s0 + 1):
                            b, e, Nb, ksl, kc0, q0, ntoe, x0 = step_geo(si)
                            h = 2 * j + e
                            sbk = si % 3
                            if ntoe > 0:
                                def mb(sbk=sbk, h=h, ntoe=ntoe, x0=x0):
                                    T_.matmul(banks[sbk][:, 0:ntoe], lhsT=identb[:], rhs=Thi[:, h, x0:x0 + ntoe], start=True, stop=False, skip_group_check=True)
                                    return T_.matmul(banks[sbk][:, 0:ntoe], lhsT=identb[:], rhs=Tlo[:, h, x0:x0 + ntoe], start=False, stop=False, skip_group_check=True)
                                P.op("pe", mb, reads=[("Thi",), ("Tlo",), ("ident",)], writes=[("bank", sbk)])
                        for si in (s0, s0 + 1):
                            b, e, Nb, ksl, kc0, q0, ntoe, x0 = step_geo(si)
                            sbk = si % 3
                            P.op("pe", lambda: T_.matmul(banks[sbk][:, 0:Nb], lhsT=KT[e * 64:(e + 1) * 64, j, kc0:kc0 + 128],
                                                         rhs=QT[e * 64:(e + 1) * 64, j, q0:q0 + Nb], start=(ntoe == 0), stop=True, skip_group_check=True),
                                 reads=[("KT", ksl, j), ("QT", j)], writes=[("bank", sbk)])
                        for si in (s0, s0 + 1):
                            b, e, Nb, ksl, kc0, q0, ntoe, x0 = step_geo(si)
                            h = 2 * j + e
                            sbk = si % 3
                            pts = si % 4
                            rdS = [("bank", sbk), ("cb",)]
                            if ntoe > 0:
                                P.op("act", lambda: S_.activation(out=PT[:, pts, 0:ntoe], in_=banks[sbk][:, 0:ntoe], func=AF.Exp), reads=[("bank", sbk)], writes=[("PT", pts)])
                            if b >= 4:
                                if ntoe < Nb:
                                    P.op("act", lambda: S_.activation(out=PT[:, pts, ntoe:Nb], in_=banks[sbk][:, ntoe:Nb], func=AF.Exp, bias=cb[:, h:h + 1]),
                                         reads=rdS, writes=[("PT", pts)])
                            else:
                                if Nb - 64 > ntoe:
                                    P.op("act", lambda: S_.activation(out=PT[:, pts, ntoe:Nb - 64], in_=banks[sbk][:, ntoe:Nb - 64], func=AF.Exp, bias=cb[:, h:h + 1]),
                                         reads=rdS, writes=[("PT", pts)])
                                P.op("act", lambda: S_.activation(out=PT[:, pts, Nb - 64:Nb], in_=banks[sbk][:, Nb - 64:Nb], func=AF.Exp, bias=cbm[:, h:h + 1]),
                                     reads=rdS + [("cbm",)], writes=[("PT", pts)])

                    def do_zero(si, j=j):
                        b, e = steps[si]
                        cq_lo = max(0, 2 * b - 8); cq_hi = min(7, 2 * b + 1)
                        Nb = 64 * (cq_hi - cq_lo + 1)
                        pts = si % 4
                        if b >= 4:
                            P.op("pool", lambda: G_.memset(PT[64:128, pts, 0:64], 0.0), reads=[], writes=[("PT", pts)])
                        else:
                            P.op("pool", lambda: G_.memset(PT[0:64, pts, Nb - 64:Nb], 0.0), reads=[], writes=[("PT", pts)])

                    def do_pv(si, j=j):
                        b, e = steps[si]
                        cq_lo = max(0, 2 * b - 8); cq_hi = min(7, 2 * b + 1)
                        Nb = 64 * (cq_hi - cq_lo + 1)
                        q0 = 64 * cq_lo
                        vslot = (4 * g - 4 + b) % 8
                        pts = si % 4
                        yb = ybank[e]
                        lhs = Vp[:, vslot, j, 0:128] if e == 0 else Vp[:, vslot, j, 64:192]
                        first = (si < 2); last = (si >= len(steps) - 2)
                        P.op("pe", lambda: T_.matmul(banks[yb][:, q0:q0 + Nb], lhsT=lhs, rhs=PT[:, pts, 0:Nb], start=first, stop=last, skip_group_check=True),
                             reads=[("PT", pts), ("Vp", vslot, 0), ("Vp", vslot, 1), ("Vones",)], writes=[("bank", yb)])

                    LOOK = 2
                    for si in range(len(steps) + LOOK):
                        if si < len(steps) and si % 2 == 0:
                            do_qk2(si)
                        if si >= LOOK:
                            do_pv(si - LOOK)
                        if pend_norm[0] is not None and si % 2 == 1:
                            try:
                                next(pend_norm[0])
                            except StopIteration:
                                pend_norm[0] = None
                        if si % 2 == 0:
                            adv_a()
                    while pend_norm[0] is not None:
                        try:
                            next(pend_norm[0])
                        except StopIteration:
                            pend_norm[0] = None
                    pend_norm[0] = normalize_pair(j, ybank[0], ybank[1], 512, 0)
                for _ in pend_norm[0]:
                    pass
                pend_norm[0] = None
                while a_pending or a_active:
                    adv_a()
                _chk(5)
                att_ssq(512, 4)
                _chk(5.5)
                ys = g % 2
                Ms = []
                for t in range(4):
                    r0 = tok0 + t * 128
                    Ms.append((merge_ple(t, xp[r0:r0 + 128, :], pp[r0:r0 + 128, :], yp[r0:r0 + 128, :], t % 2, ys, (2 * (t % 2), 2 * (t % 2) + 1)),
                               {("s", t % 2)}))
                if g + 1 < NG:
                    gens = [(proj_chain(g + 1, [4, 5, 6]), {("p",)})] + Ms
                else:
                    gens = [(sample_chain([4, 5, 6]), {("p",)})] + Ms
                run_pipelined(gens, 7, 3)
                _chk(6 + 0.01 * g)

            _chk(7)
            _chk(7.05)
            _chk(7.2)

            def prep(s):
                kh = s % 2
                for t in range(4):
                    sl = t % 2
                    dma("sp", f"xin{sl}", xin[:, sl, 0:512], ck[s, t * 128:(t + 1) * 128, :], writes=[("xin", sl)])
                    dma("sp", f"xinb{sl}", xin[:, sl, 512:1024], cv[s, t * 128:(t + 1) * 128, :], writes=[("xinb", sl)])
                    P.op("dve", lambda: V_.tensor_copy(out=xsb2[:, 0, 0:512], in_=xin[:, sl, 0:512]), reads=[("xin", sl)], writes=[("xsb", 0)])

                    def trk():
                        last = None
                        for jj in range(4):
                            last = T_.transpose(trb[:, jj * 128:(jj + 1) * 128], xsb2[:, 0, jj * 128:(jj + 1) * 128], identb[:])
                        return last
                    P.op("pe", trk, reads=[("xsb", 0), ("ident",)], writes=[("bank", 7)])
                    c0 = kh * 512 + t * 128
                    P.op("act", lambda: S_.copy(out=KT[:, :, c0:c0 + 128], in_=trb[:, 0:512].rearrange("p (c t) -> p c t", c=4)),
                         reads=[("bank", 7)], writes=[("KT", kh, jj) for jj in range(4)])
                    srcv = xin[:, sl, 512:1024].rearrange("p (j e d) -> p j e d", e=2, d=64)
                    vs_ = 4 * kh + t
                    P.op("act", lambda: S_.copy(out=Vp[:, vs_, :, 0:64], in_=srcv[:, :, 0, :]), reads=[("xinb", sl), ("Vones",)], writes=[("Vp", vs_, 0)])
                    P.op("dve", lambda: V_.tensor_copy(out=Vp[:, vs_, :, 128:192], in_=srcv[:, :, 1, :]), reads=[("xinb", sl), ("Vones",)], writes=[("Vp", vs_, 1)])

            def attend(s):
                kh = s % 2
                q0 = 32 * s
                nv = newV(s)

                def blk(kb):
                    KP = 128 if kb < 4 else 32
                    sbe = (0, 1) if kb % 2 == 0 else (2, 5)
                    pts = kb % 3

                    def qk():
                        last = None
                        for h in range(8):
                            jj, e = h // 2, h % 2
                            if kb < 4:
                                lhs = KT[e * 64:(e + 1) * 64, jj, kh * 512 + kb * 128:kh * 512 + (kb + 1) * 128]
                            else:
                                lhs = zas[e * 64:(e + 1) * 64, jj, 128 + 32 * s:128 + 32 * (s + 1)]
                            last = T_.matmul(banks[sbe[e]][0:KP, jj * 32:(jj + 1) * 32], lhsT=lhs,
                                             rhs=QT[e * 64:(e + 1) * 64, jj, q0:q0 + 32], start=(h < 2), stop=(h >= 6), skip_group_check=True)
                        return last
                    P.op("pe", qk, reads=[("KT", kh, jj) for jj in range(4)] + [("Knew", jj) for jj in range(4)] + [("QT", jj) for jj in range(4)],
                         writes=[("bank", sbe[0]), ("bank", sbe[1])])
                    for e in range(2):
                        S3 = banks[sbe[e]][0:KP, 0:128].rearrange("p (j q) -> p j q", q=32)
                        Sb3 = Sb[0:KP, e, 0:128].rearrange("p (j q) -> p j q", q=32)
                        if kb < 3:
                            bsrc = cb[0:KP, :].rearrange("p (j e) -> p j e", e=2)[:, :, e].unsqueeze(2).to_broadcast([KP, 4, 32])
                            blo = None
                        elif kb == 3:
                            bsrc = Thi[:, :, 128:160].rearrange("p (j e) q -> p j e q", e=2)[:, :, e, :]
                            blo = Tlo[:, :, 128:160].rearrange("p (j e) q -> p j e q", e=2)[:, :, e, :]
                        else:
                            bsrc = Thi[0:32, :, 0:32].rearrange("p (j e) q -> p j e q", e=2)[:, :, e, :]
                            blo = Tlo[0:32, :, 0:32].rearrange("p (j e) q -> p j e q", e=2)[:, :, e, :]
                        P.op("dve", lambda: V_.tensor_tensor(out=Sb3, in0=S3, in1=bsrc, op=ALU.add),
                             reads=[("bank", sbe[e]), ("Thi",), ("cb",)], writes=[("Sb", e)])
                        if blo is not None:
                            P.op("dve", lambda: V_.tensor_tensor(out=Sb3, in0=Sb3, in1=blo, op=ALU.add),
                                 reads=[("Sb", e), ("Tlo",)], writes=[("Sb", e)])
                        P.op("act", lambda: S_.activation(out=PT[0:KP, pts, e * 128:(e + 1) * 128], in_=Sb[0:KP, e, 0:128], func=AF.Exp),
                             reads=[("Sb", e)], writes=[("PT", pts, e)])

                    def pv():
                        last = None
                        for h in range(8):
                            jj, e = h // 2, h % 2
                            if kb < 4:
                                vt = Vp[0:KP, 4 * kh + kb, jj, :]
                            else:
                                vt = nv[:, jj, :]
                            lhs = vt[:, 0:128] if e == 0 else vt[:, 64:192]
                            last = T_.matmul(banks[3 + e][:, jj * 32:(jj + 1) * 32], lhsT=lhs, rhs=PT[0:KP, pts, e * 128 + jj * 32:e * 128 + (jj + 1) * 32],
                                             start=(kb == 0 and h < 2), stop=(kb == 4), skip_group_check=True)
                        return last
                    vrd = [("Vp", 4 * kh + kb, e) for e in range(2)] if kb < 4 else [("Vnew", s, i) for i in range(3)]
                    return lambda: P.op("pe", pv, reads=[("PT", pts), ("PT", pts, 0), ("PT", pts, 1), ("Vones",)] + vrd,
                                        writes=[("bank", 3), ("bank", 4), ("PT", pts)])
                pvs = []
                for kb in range(5):
                    pvs.append(blk(kb))
                    if kb >= 1:
                        pvs[kb - 1]()
                pvs[4]()
                A, B = banks[3], banks[4]
                P.op("dve", lambda: V_.reciprocal(out=Rr[0:64, 0:128], in_=A[64:128, 0:128]), reads=[("bank", 3)], writes=[("Rr", 0), ("xs1T", 1)])
                P.op("dve", lambda: V_.reciprocal(out=Rr[64:128, 0:128], in_=B[0:64, 0:128]), reads=[("bank", 4)], writes=[("Rr", 1), ("xs1T", 1)])
                P.op("dve", lambda: V_.tensor_tensor(out=yan[0:64, 0:128], in0=A[0:64, 0:128], in1=Rr[0:64, 0:128], op=ALU.mult),
                     reads=[("bank", 3), ("Rr", 0), ("xs1T", 1)], writes=[("yan", 0), ("xs1T", 0)])
                P.op("dve", lambda: V_.tensor_tensor(out=yan[64:128, 0:128], in0=B[64:128, 0:128], in1=Rr[64:128, 0:128], op=ALU.mult),
                     reads=[("bank", 4), ("Rr", 1), ("xs1T", 1)], writes=[("yan", 1), ("xs1T", 0)])
                P.op("dve", lambda: V_.tensor_tensor(out=yaT[:, :, q0:q0 + 32], in0=yan[:, 0:128].rearrange("p (j q) -> p j q", q=32),
                                                     in1=zas[:, :, q0:q0 + 32], op=ALU.mult),
                     reads=[("yan", 0), ("yan", 1), ("xs1T", 0)] + [("zas", jj) for jj in range(4)], writes=[("yaT", jj) for jj in range(4)])

            prep(0)
            prep(1)
            attend(0)
            prep(2)
            attend(1)
            prep(3)
            attend(2)
            attend(3)
            _chk(7.5)
            att_ssq(128, 1)
            run_pipelined([(merge_ple(0, xsm, psm, ysm, 0, 0, None), {("s", 0)})], 1, 1)

        except _Stop:
            pass
        P.finish("sp")
    return nc


_NC_CACHE = {}


def kernel(x_prompt, x_sample, cache_k, cache_v, state_conv, p_prompt, p_sample, norm_in, w_in, conv_w, rel_bias,
           norm_conv, norm_att, w_out, ple_norm, w_ple_gate, w_ple_proj, final_norm):
    f = lambda a: np.ascontiguousarray(np.asarray(a, dtype=np.float32))
    x_prompt, x_sample, cache_k, cache_v, state_conv = map(f, (x_prompt, x_sample, cache_k, cache_v, state_conv))
    p_prompt, p_sample = f(p_prompt), f(p_sample)
    if "nc" not in _NC_CACHE:
        _NC_CACHE["nc"] = build_program()
    nc = _NC_CACHE["nc"]
    shared = dict(norm_in=f(norm_in)[0], w_in=f(w_in)[0], conv_w=f(conv_w)[0], rel_bias=f(rel_bias)[0], norm_conv=f(norm_conv)[0],
                  norm_att=f(norm_att)[0], w_out=f(w_out)[0], ple_norm=f(ple_norm)[0], w_g=f(w_ple_gate)[0], w_p=f(w_ple_proj)[0],
                  final_norm=f(final_norm), ident=np.eye(128, dtype=np.float32))
    in_maps = []
    for c in range(N_CORES):
        m = dict(shared)
        m["xp"] = x_prompt[c]
        m["pp"] = p_prompt[0, c]
        m["xsm"] = x_sample[4 * c:4 * c + 4].reshape(128, 1024)
        m["psm"] = p_sample[0, 4 * c:4 * c + 4].reshape(128, 256)
        m["ck"] = cache_k[0, 4 * c:4 * c + 4].reshape(4, 512, 512)
        m["cv"] = cache_v[0, 4 * c:4 * c + 4].reshape(4, 512, 512)
        m["sc"] = state_conv[0, 4 * c:4 * c + 4]
        in_maps.append({k: np.ascontiguousarray(v) for k, v in m.items()})
    res = run_bass_kernel_spmd(nc, in_maps, core_ids=list(range(N_CORES)))
    R = res.results
    y_prompt = np.stack([R[c]["yp"] for c in range(N_CORES)]).astype(np.float32)
    y_sample = np.concatenate([R[c]["ysm"].reshape(4, 32, 1024) for c in range(N_CORES)]).astype(np.float32)
    k_prompt = np.stack([R[c]["kp"].reshape(512, 8, 64) for c in range(N_CORES)])[None].astype(np.float32)
    v_prompt = np.stack([R[c]["vp"].reshape(512, 8, 64) for c in range(N_CORES)])[None].astype(np.float32)
    conv_prompt = np.stack([R[c]["cp"] for c in range(N_CORES)])[None].astype(np.float32)
    k_sample = np.concatenate([R[c]["ksn"].reshape(4, 32, 8, 64) for c in range(N_CORES)])[None].astype(np.float32)
    v_sample = np.concatenate([R[c]["vsn"].reshape(4, 32, 8, 64) for c in range(N_CORES)])[None].astype(np.float32)
    conv_sample = np.concatenate([R[c]["csn"] for c in range(N_CORES)])[None].astype(np.float32)
    return (y_prompt, y_sample, k_prompt, v_prompt, conv_prompt, k_sample, v_sample, conv_sample)
```
